# Optimizing a Trainium2 kernel written in Bass

```python
import jax, jax.numpy as jnp
from jax import lax
import numpy as np

D_MODEL = 1024
BATCH = 8
SEQ = 2048
DEPTH = 4

CONV_WIDTH = D_MODEL // 2
CONV_HEADS = 8
CONV_K = 3
POOL_WIDTH = D_MODEL // 2
POOL_WINDOWS = (2, 4, 8, 16)
POOL_GROUPS = len(POOL_WINDOWS)
POOL_GROUP_DIM = POOL_WIDTH // POOL_GROUPS
MIX_WIDTH = CONV_WIDTH + POOL_WIDTH
IN_COLS = 4 * CONV_WIDTH + 2 * POOL_WIDTH
NORM_EPS = 1e-6

kernel_name = "hybrid_shortconv_pool_parallel_adaln"


def rms_norm(x, g):
    xf = x.astype(jnp.float32)
    y = xf * lax.rsqrt(jnp.mean(xf * xf, axis=-1, keepdims=True) + NORM_EPS)
    return (y * g.astype(jnp.float32)).astype(x.dtype)


def causal_depthwise_conv(z, w):
    C = z.shape[-1]
    return lax.conv_general_dilated(
        z, w[:, None, :].astype(z.dtype),
        window_strides=(1,), padding=[(CONV_K - 1, 0)],
        dimension_numbers=("NWC", "WIO", "NWC"),
        feature_group_count=C)


def causal_multiscale_pool(p, w_pool, pool_scale):
    B, T, _ = p.shape
    pf = p.astype(jnp.float32)
    pos = jnp.arange(1, T + 1, dtype=jnp.float32)[None, :, None]
    outs = []
    for g, w in enumerate(POOL_WINDOWS):
        pg = pf[..., g * POOL_GROUP_DIM:(g + 1) * POOL_GROUP_DIM]
        s = jnp.cumsum(pg, axis=1)
        lag = jnp.pad(s, ((0, 0), (w, 0), (0, 0)))[:, :T]
        mean = (s - lag) / jnp.minimum(pos, float(w))
        outs.append(mean - pg)
    pooled = jnp.stack(outs, axis=2).astype(p.dtype)
    mixed = jnp.einsum("btgc,gcd->btgd", pooled, w_pool)
    return mixed.reshape(B, T, POOL_WIDTH) * pool_scale


def setup_inputs(seed: int = 0) -> dict:
    key = jax.random.key(seed)
    ks = jax.random.split(key, 12)
    f32 = jnp.float32
    x = jax.random.normal(ks[0], (BATCH, SEQ, D_MODEL), f32)
    c = jax.random.normal(ks[1], (BATCH, D_MODEL), f32)
    w_ada = jax.random.normal(ks[2], (DEPTH, D_MODEL, 3 * D_MODEL), f32) * D_MODEL ** -0.5
    b_ada = 0.01 * jax.random.normal(ks[3], (DEPTH, 3 * D_MODEL), f32)
    g_pre = 1.0 + 0.02 * jax.random.normal(ks[4], (DEPTH, D_MODEL), f32)
    w_in = jax.random.normal(ks[5], (DEPTH, D_MODEL, IN_COLS), f32) * D_MODEL ** -0.5
    w_conv = jax.random.normal(ks[6], (DEPTH, CONV_K, CONV_WIDTH), f32) * CONV_K ** -0.5
    w_pool = jax.random.normal(ks[7], (DEPTH, POOL_GROUPS, POOL_GROUP_DIM, POOL_GROUP_DIM), f32) * POOL_GROUP_DIM ** -0.5
    pool_scale = 1.0 + 0.1 * jax.random.normal(ks[8], (DEPTH, POOL_WIDTH), f32)
    w_out = jax.random.normal(ks[9], (DEPTH, MIX_WIDTH, D_MODEL), f32) * MIX_WIDTH ** -0.5
    g_post = 1.0 + 0.02 * jax.random.normal(ks[10], (DEPTH, D_MODEL), f32)
    return {"x": x, "c": c, "w_ada": w_ada, "b_ada": b_ada, "g_pre": g_pre, "w_in": w_in,
            "w_conv": w_conv, "w_pool": w_pool, "pool_scale": pool_scale,
            "w_out": w_out, "g_post": g_post}


def reference(x, c, w_ada, b_ada, g_pre, w_in, w_conv, w_pool, pool_scale, w_out, g_post):
    B, T, D = x.shape
    c_act = jax.nn.silu(c)
    cw = CONV_WIDTH
    for l in range(DEPTH):
        mod = c_act @ w_ada[l] + b_ada[l]
        shift, scale, gate = jnp.split(mod, 3, axis=-1)
        h = rms_norm(x, g_pre[l]) * (1.0 + scale[:, None, :]) + shift[:, None, :]
        proj = h @ w_in[l]
        u_a = proj[..., 0 * cw:1 * cw]
        b_a = proj[..., 1 * cw:2 * cw]
        c_a = proj[..., 2 * cw:3 * cw]
        gate_a = proj[..., 3 * cw:4 * cw]
        o = 4 * cw
        u_p = proj[..., o:o + POOL_WIDTH]
        gate_p = proj[..., o + POOL_WIDTH:o + 2 * POOL_WIDTH]
        y_a = b_a * causal_depthwise_conv(c_a * u_a, w_conv[l]) * jax.nn.silu(gate_a)
        y_p = causal_multiscale_pool(u_p, w_pool[l], pool_scale[l]) * jax.nn.silu(gate_p)
        y = jnp.concatenate([y_a, y_p], axis=-1) @ w_out[l]
        x = x + gate[:, None, :] * rms_norm(y, g_post[l])
    return x
```

```python
import numpy as np
from contextlib import ExitStack

import concourse.bass as bass
import concourse.mybir as mybir
from concourse.bass_utils import run_bass_kernel_spmd

F32 = mybir.dt.float32
BF16 = mybir.dt.bfloat16
AF = mybir.ActivationFunctionType
ALU = mybir.AluOpType

D = 1024
KC = D // 128
NCOLS = 3072
NJ = NCOLS // 128
DEPTH = 4
SEQ = 2048
NCORES = 8
EPS = 1e-6
TN = 512
WN = 512
VW = 56
POOL_W = (2, 4, 8, 16)
NTMP = 3
ADA_W = 128
NADA = NCOLS // ADA_W


class Sched:
    def __init__(self, nc, es):
        self.nc = nc
        self.es = es
        self.prog = {e: [] for e in ("tensor", "vector", "scalar", "gpsimd", "sync")}
        self.sem = {}
        self.count = {}
        for e in self.prog:
            self.sem[e] = es.enter_context(nc.semaphore("s_" + e))
            self.count[e] = 0
        self.waited = {e: {} for e in self.prog}
        self.last_w = {}
        self.readers = {}
        self.nslots = 0
        self.semname = {}

    def slot(self, name):
        s = self.es.enter_context(self.nc.semaphore("d_" + name))
        self.nslots += 1
        return {"sem": s, "count": 0, "name": "d_" + name}

    def _deps(self, eng, reads, writes):
        need = {}

        def add(ev):
            if ev is None:
                return
            name, sem, val = ev
            if name not in need or need[name][1] < val:
                need[name] = (sem, val)

        for k in reads:
            add(self.last_w.get(k))
        for k in writes:
            add(self.last_w.get(k))
            for ev in self.readers.get(k, ()):
                add(ev)
        for name, (sem, val) in need.items():
            if eng == "tensor" and name == "tensor":
                continue
            if self.waited[eng].get(name, 0) < val:
                self.waited[eng][name] = val
                self.prog[eng].append(lambda e, sem=sem, val=val: e.wait_ge(sem, val))

    def _commit(self, ev, reads, writes):
        for k in writes:
            self.last_w[k] = ev
            self.readers[k] = []
        for k in reads:
            self.readers.setdefault(k, []).append(ev)

    def op(self, eng, fn, reads=(), writes=()):
        self._deps(eng, reads, writes)
        self.count[eng] += 1
        sem = self.sem[eng]
        self.prog[eng].append(lambda e, fn=fn, sem=sem: fn(e).then_inc(sem, 1))
        self._commit((eng, sem, self.count[eng]), reads, writes)

    def dma(self, queue, slot, out, in_, reads=(), writes=()):
        self._deps(queue, reads, writes)
        slot["count"] += 16
        sem = slot["sem"]
        self.prog[queue].append(lambda e, out=out, in_=in_, sem=sem: e.dma_start(out=out, in_=in_).then_inc(sem, 16))
        self._commit((slot["name"], sem, slot["count"]), reads, writes)

    def wait_all(self, eng, slots):
        for s in slots:
            if s["count"] > 0:
                self.prog[eng].append(lambda e, sem=s["sem"], v=s["count"]: e.wait_ge(sem, v))

    def emit(self):
        nc = self.nc
        with nc.Block() as block:
            @block.tensor
            def _(e):
                for f in self.prog["tensor"]:
                    f(e)

            @block.vector
            def _(e):
                for f in self.prog["vector"]:
                    f(e)

            @block.scalar
            def _(e):
                for f in self.prog["scalar"]:
                    f(e)

            @block.gpsimd
            def _(e):
                for f in self.prog["gpsimd"]:
                    f(e)

            @block.sync
            def _(e):
                for f in self.prog["sync"]:
                    f(e)


def build(L=DEPTH, T=SEQ):
    NT = T // TN
    nc = bass.Bass("TRN2", target_bir_lowering=False)
    xT = nc.dram_tensor("xT", [D, T], F32, kind="ExternalInput").ap()
    cT = nc.dram_tensor("cT", [128, KC], F32, kind="ExternalInput").ap()
    vecs_d = nc.dram_tensor("vecs", [128, L * VW], F32, kind="ExternalInput").ap()
    w_ada = nc.dram_tensor("w_ada", [L, D, NCOLS], F32, kind="ExternalInput").ap()
    w_in = nc.dram_tensor("w_in", [L, D, NCOLS], F32, kind="ExternalInput").ap()
    w_out = nc.dram_tensor("w_out", [L, D, D], F32, kind="ExternalInput").ap()
    w_pool = nc.dram_tensor("w_pool", [L, 4, 128, 128], F32, kind="ExternalInput").ap()
    outT = nc.dram_tensor("outT", [D, T], F32, kind="ExternalOutput").ap()

    with ExitStack() as es:
        def sb(name, shape, dt):
            return es.enter_context(nc.sbuf_tensor(name, shape, dt))

        def pt(name):
            return es.enter_context(nc.psum_tensor(name, [128, 512], F32))

        S = Sched(nc, es)

        x_res = sb("x_res", [128, KC, T], F32)
        win_sb = sb("win_sb", [128, KC, NCOLS], BF16)
        wout_sb = sb("wout_sb", [128, KC, D], BF16)
        wpool_sb = sb("wpool_sb", [128, 4, 128], BF16)
        wada_sb = [sb(f"wada{i}", [128, KC, ADA_W], BF16) for i in range(2)]
        vecs = sb("vecs_sb", [128, L * VW], F32)
        c_sb = sb("c_sb", [128, KC], F32)
        cact = sb("cact", [128, KC], BF16)
        ones_bf = sb("ones_bf", [128, 128], BF16)
        epsb = sb("epsb", [128, 1], F32)
        dmy_in = sb("dmy_in", [128, 1], F32)
        dmy_out = sb("dmy_out", [128, 1], F32)
        modv = sb("modv", [128, 24], F32)
        A_sb = [sb(f"A{i}", [128, KC], F32) for i in range(2)]
        G_sb = [sb(f"G{i}", [128, KC], F32) for i in range(2)]
        Sh_sb = [sb(f"Sh{i}", [128, KC], F32) for i in range(2)]
        t8 = sb("t8", [128, KC], F32)
        io16 = sb("io16", [128, 16], F32)
        rvec = sb("rvec", [128, 4, 16], F32)
        t16 = sb("t16", [128, 16], F32)
        xsq = [sb(f"xsq{i}", [128, TN], BF16) for i in range(3)]
        ssb1 = sb("ssb1", [128, TN], F32)
        ssb2 = ssb1
        h_sb = sb("h_sb", [128, KC, TN], BF16)
        tmp = [sb(f"tmp{i}", [128, TN], F32) for i in range(NTMP)]
        acc = [sb(f"acc{i}", [128, TN], F32) for i in range(4)]
        cu = [sb(f"cu{i}", [128, 2 + TN], F32) for i in range(2)]
        hist_cu = sb("hist_cu", [128, 4, 2], F32)
        U = [sb(f"U{i}", [128, 15 + TN], F32) for i in range(2)]
        Pb = sb("Pb", [128, 15 + TN], F32)
        Qb = sb("Qb", [128, 15 + TN], F32)
        hist_u = sb("hist_u", [128, 4, 15], F32)
        pooled = [sb(f"pooled{i}", [128, TN], BF16) for i in range(4)]
        ycat = sb("ycat", [128, KC, TN], BF16)
        y_sb = sb("y_sb", [128, KC, WN], F32)
        ysq = xsq

        NB = 6
        banks = [pt(f"pj{i}") for i in range(NB)]
        st1 = pt("st1")
        st2 = st1
        modps = pt("modps")

        sl_x = [S.slot(f"x{i}") for i in range(NT)]
        sl_vec = S.slot("vec")
        sl_c = S.slot("c")
        sl_win = [S.slot(f"win{i}") for i in range(6)]
        sl_wout = [S.slot(f"wout{i}") for i in range(4)]
        sl_wpool = S.slot("wpool")
        sl_wada = [S.slot(f"wada{i}") for i in range(2)]
        sl_out = S.slot("out")

        state = {"bank": 0, "tmp": 0, "xsq": 0, "ysq": 0, "cu": 0, "U": 0}

        busy = {}

        def rot(kind, n):
            i = state[kind]
            state[kind] = (i + 1) % n
            assert not busy.get((kind, i), False), f"rotating buffer {kind}[{i}] re-allocated before its consumer was emitted"
            return i

        def hold(kind, i):
            busy[(kind, i)] = True

        def release(kind, i):
            busy[(kind, i)] = False

        ticks = {"n": 0, "seq": 0}
        deferred = []

        def defer(k, fn):
            ticks["seq"] += 1
            deferred.append((ticks["n"] + k, ticks["seq"], fn))

        def run_due(limit):
            while True:
                due = [d for d in deferred if d[0] <= limit]
                if not due:
                    return
                d = min(due)
                deferred.remove(d)
                d[2]()

        def tick():
            ticks["n"] += 1
            run_due(ticks["n"])

        def flush():
            run_due(1 << 60)

        def pull_table(func):
            S.op("scalar", lambda e, func=func: e.activation(out=dmy_out[:, :], in_=dmy_in[:, :], func=func),
                 reads=[("dmy_in",)], writes=[("dmy_out",)])

        def xkeys(c, tt):
            return [("x", c, 2 * tt), ("x", c, 2 * tt + 1)]

        def load_win_piece(l, p):
            src = w_in[l].rearrange("(k p) c -> p k c", p=128)[:, :, p * 512:(p + 1) * 512]
            S.dma("gpsimd", sl_win[p], win_sb[:, :, p * 512:(p + 1) * 512], src,
                  writes=[("win", p)])

        def load_wout_piece(l, q):
            src = w_out[l].rearrange("(k p) c -> p k c", p=128)[:, :, q * 256:(q + 1) * 256]
            S.dma("gpsimd", sl_wout[q], wout_sb[:, :, q * 256:(q + 1) * 256], src,
                  writes=[("wout", q)])

        def load_wpool(l):
            src = w_pool[l].rearrange("g c d -> c g d")
            S.dma("gpsimd", sl_wpool, wpool_sb[:, :, :], src, writes=[("wpool",)])

        ada_state = {"loaded": 0, "done": 0}

        def load_wada_next():
            n = ada_state["loaded"]
            if n >= NADA * L:
                return
            l, p = divmod(n, NADA)
            slot = n % 2
            src = w_ada[l].rearrange("(k p) c -> p k c", p=128)[:, :, p * ADA_W:(p + 1) * ADA_W]
            S.dma("gpsimd", sl_wada[slot], wada_sb[slot][:, :, :], src, writes=[("wada", slot)])
            ada_state["loaded"] = n + 1

        def mod_piece():
            n = ada_state["done"]
            l, p = divmod(n, NADA)
            slot = n % 2

            def fn(e, slot=slot, p=p):
                ins = None
                for k in range(KC):
                    ins = e.matmul(modps[:, p:p + 1], lhsT=wada_sb[slot][:, k, :],
                                   rhs=cact[:, k:k + 1], start=(k == 0), stop=(k == KC - 1))
                return ins
            S.op("tensor", fn, reads=[("wada", slot), ("cact",)], writes=[("modps", p)])
            ada_state["done"] = n + 1
            load_wada_next()

        def mod_finish_AS(l):
            v0 = l * VW
            par = l % 2
            S.op("vector", lambda e: e.tensor_tensor(out=modv[:, 0:16], in0=modps[:, 0:16], in1=vecs[:, v0:v0 + 16], op=ALU.add),
                 reads=[("modps", p) for p in range(NADA)] + [("vecs",)], writes=[("modv", 0)])
            S.op("vector", lambda e: e.tensor_copy(out=Sh_sb[par][:, :], in_=modv[:, 0:8]),
                 reads=[("modv", 0)], writes=[("Sh", par)])
            S.op("vector", lambda e: e.tensor_scalar(out=t8[:, :], in0=modv[:, 8:16], scalar1=1.0, scalar2=None, op0=ALU.add),
                 reads=[("modv", 0)], writes=[("t8",)])
            S.op("vector", lambda e: e.scalar_tensor_tensor(out=A_sb[par][:, :], in0=t8[:, :], scalar=32.0, in1=vecs[:, v0 + 24:v0 + 32],
                                                            op0=ALU.mult, op1=ALU.mult),
                 reads=[("t8",), ("vecs",)], writes=[("A", par)])

        def mod_finish_G(l):
            v0 = l * VW
            par = l % 2
            S.op("vector", lambda e: e.tensor_tensor(out=modv[:, 16:24], in0=modps[:, 16:24], in1=vecs[:, v0 + 16:v0 + 24], op=ALU.add),
                 reads=[("modps", p) for p in range(NADA)] + [("vecs",)], writes=[("modv", 1)])
            S.op("vector", lambda e: e.scalar_tensor_tensor(out=G_sb[par][:, :], in0=modv[:, 16:24], scalar=32.0, in1=vecs[:, v0 + 32:v0 + 40],
                                                            op0=ALU.mult, op1=ALU.mult),
                 reads=[("modv", 1), ("vecs",)], writes=[("G", par)])

        def mod_finish(l):
            mod_finish_AS(l)
            mod_finish_G(l)

        def rstd_from(st, n):
            S.op("scalar", lambda e: e.activation(out=ssb1[:, 0:n], in_=st[:, 0:n], func=AF.Ln, bias=epsb[:, 0:1], scale=1.0),
                 reads=[("st", id(st)), ("epsb",)], writes=[("ssb1",)])
            S.op("scalar", lambda e: e.activation(out=ssb1[:, 0:n], in_=ssb1[:, 0:n], func=AF.Exp, scale=-0.5),
                 reads=[("ssb1",)], writes=[("ssb1",)])

        def stats1_sq(tt, c):
            t0 = tt * TN
            b = rot("xsq", 3)
            S.op("scalar", lambda e, b=b, c=c: e.activation(out=xsq[b][:, :], in_=x_res[:, c, t0:t0 + TN], func=AF.Square),
                 reads=xkeys(c, tt), writes=[("xsq", b)])
            hold("xsq", b)
            return b

        def stats1_mm(c, b):
            S.op("tensor", lambda e, b=b, c=c: e.matmul(st1[:, :], lhsT=ones_bf[:, :], rhs=xsq[b][:, :],
                                                        start=(c == 0), stop=(c == KC - 1)),
                 reads=[("xsq", b), ("ones",)], writes=[("st", id(st1))])
            release("xsq", b)
            if c == KC - 1:
                rstd_from(st1, TN)

        def stats1(l, tt):
            for c in range(KC):
                b = stats1_sq(tt, c)
                stats1_mm(c, b)

        def stats1_deferred(l, tt):
            for c in range(KC):
                def job(c=c):
                    b = stats1_sq(tt, c)
                    defer(2, lambda c=c, b=b: stats1_mm(c, b))
                defer(c + 2, job)

        def hchunk(l, tt, c, on_act=False):
            t0 = tt * TN
            par = l % 2
            ti = rot("tmp", NTMP)
            S.op("vector", lambda e, ti=ti, c=c: e.tensor_tensor(out=tmp[ti][:, :], in0=x_res[:, c, t0:t0 + TN], in1=ssb1[:, :], op=ALU.mult),
                 reads=xkeys(c, tt) + [("ssb1",)], writes=[("tmp", ti)])
            if on_act:
                S.op("scalar", lambda e, ti=ti, c=c: e.activation(out=h_sb[:, c, :], in_=tmp[ti][:, :], func=AF.Identity,
                                                                  bias=Sh_sb[par][:, c:c + 1], scale=A_sb[par][:, c:c + 1]),
                     reads=[("tmp", ti), ("A", par), ("Sh", par)], writes=[("h", c)])
                return
            S.op("vector", lambda e, ti=ti, c=c: e.tensor_scalar(out=h_sb[:, c, :], in0=tmp[ti][:, :],
                                                                 scalar1=A_sb[par][:, c:c + 1], scalar2=Sh_sb[par][:, c:c + 1],
                                                                 op0=ALU.mult, op1=ALU.add),
                 reads=[("tmp", ti), ("A", par), ("Sh", par)], writes=[("h", c)])

        def hchunk_split(l, tt, c):
            t0 = tt * TN
            par = l % 2
            ti = rot("tmp", NTMP)
            S.op("vector", lambda e, ti=ti, c=c: e.tensor_tensor(out=tmp[ti][:, :], in0=x_res[:, c, t0:t0 + TN], in1=ssb1[:, :], op=ALU.mult),
                 reads=xkeys(c, tt) + [("ssb1",)], writes=[("tmp", ti)])

            def fin(ti=ti, c=c, par=par):
                S.op("vector", lambda e, ti=ti, c=c: e.tensor_scalar(out=h_sb[:, c, :], in0=tmp[ti][:, :],
                                                                     scalar1=A_sb[par][:, c:c + 1], scalar2=Sh_sb[par][:, c:c + 1],
                                                                     op0=ALU.mult, op1=ALU.add),
                     reads=[("tmp", ti), ("A", par), ("Sh", par)], writes=[("h", c)])
            return fin

        def hphase(l, tt):
            for c in range(KC):
                hchunk(l, tt, c, on_act=True)

        def proj_group(l, tt, j, last_tile):
            b = rot("bank", NB)
            p = j // 4

            def fn(e, b=b, j=j):
                ins = None
                for k in range(KC):
                    ins = e.matmul(banks[b][:, :], lhsT=win_sb[:, k, j * 128:(j + 1) * 128], rhs=h_sb[:, k, :],
                                   start=(k == 0), stop=(k == KC - 1))
                return ins
            S.op("tensor", fn, reads=[("win", p)] + [("h", c) for c in range(KC)], writes=[("bank", b)])
            return b

        def win_phase(l, tt):
            v0 = l * VW
            first = (tt == 0)
            last_tile = (tt == NT - 1) and (l + 1 < L)
            for g in range(4):
                w = POOL_W[g]
                b = proj_group(l, tt, 16 + g, last_tile)
                ui = rot("U", 2)
                Ub = U[ui]
                if first:
                    S.op("vector", lambda e, Ub=Ub: e.memset(Ub[:, 0:15], 0.0), writes=[("U", ui)])
                else:
                    S.op("scalar", lambda e, Ub=Ub, g=g: e.activation(out=Ub[:, 0:15], in_=hist_u[:, g, :], func=AF.Copy),
                         reads=[("hist_u", g)], writes=[("U", ui)])
                S.op("scalar", lambda e, Ub=Ub, b=b: e.activation(out=Ub[:, 15:15 + TN], in_=banks[b][:, :], func=AF.Copy),
                     reads=[("bank", b)], writes=[("Ucur", ui)])
                def save_hist(Ub=Ub, g=g, ui=ui):
                    S.op("scalar", lambda e, Ub=Ub, g=g: e.activation(out=hist_u[:, g, :], in_=Ub[:, TN:TN + 15], func=AF.Copy),
                         reads=[("Ucur", ui)], writes=[("hist_u", g)])
                defer(1, save_hist)
                src, skey = Ub, [("U", ui), ("Ucur", ui)]
                off = 0
                dsts = [(Pb, ("Pb",)), (Qb, ("Qb",))]
                for step in range(g + 1):
                    sh = 1 << step
                    dst, dkey = dsts[step % 2]
                    lo = off + sh
                    S.op("vector", lambda e, dst=dst, src=src, lo=lo, sh=sh: e.tensor_tensor(
                        out=dst[:, lo:15 + TN], in0=src[:, lo:15 + TN], in1=src[:, lo - sh:15 + TN - sh], op=ALU.add),
                        reads=skey, writes=[dkey])
                    src, skey, off = dst, [dkey], lo
                S.op("vector", lambda e, src=src, Ub=Ub, g=g, w=w: e.scalar_tensor_tensor(
                    out=pooled[g][:, :], in0=src[:, 15:15 + TN], scalar=1.0 / w, in1=Ub[:, 15:15 + TN],
                    op0=ALU.mult, op1=ALU.subtract),
                    reads=skey + [("Ucur", ui)], writes=[("pooled", g)])
                if first:
                    S.op("vector", lambda e, src=src, g=g: e.tensor_tensor(out=t16[:, :], in0=src[:, 15:31], in1=rvec[:, g, :], op=ALU.mult),
                         reads=skey + [("rvec",)], writes=[("t16",)])
                    S.op("vector", lambda e, Ub=Ub, g=g: e.tensor_tensor(out=pooled[g][:, 0:16], in0=t16[:, :], in1=Ub[:, 15:31], op=ALU.subtract),
                         reads=[("t16",), ("Ucur", ui)], writes=[("pooled", g)])
                tick()
            if last_tile:
                load_win_piece(l + 1, 4)
            accs = []
            for i in range(4):
                bu = proj_group(l, tt, i, last_tile)
                ua = rot("tmp", NTMP)
                S.op("scalar", lambda e, ua=ua, bu=bu: e.activation(out=tmp[ua][:, :], in_=banks[bu][:, :], func=AF.Copy),
                     reads=[("bank", bu)], writes=[("tmp", ua)])
                tick()
                bc = proj_group(l, tt, 8 + i, last_tile)
                ci = rot("cu", 2)
                cb = cu[ci]
                if first:
                    S.op("vector", lambda e, cb=cb: e.memset(cb[:, 0:2], 0.0), writes=[("cuh", ci)])
                else:
                    S.op("vector", lambda e, cb=cb, i=i: e.tensor_copy(out=cb[:, 0:2], in_=hist_cu[:, i, :]),
                         reads=[("hist_cu", i)], writes=[("cuh", ci)])
                S.op("vector", lambda e, cb=cb, bc=bc, ua=ua: e.tensor_tensor(out=cb[:, 2:2 + TN], in0=banks[bc][:, :], in1=tmp[ua][:, :], op=ALU.mult),
                     reads=[("bank", bc), ("tmp", ua)], writes=[("cu", ci)])
                ac = i
                wc = v0 + 40 + 3 * i
                S.op("vector", lambda e, cb=cb, ac=ac, wc=wc: e.tensor_scalar(out=acc[ac][:, :], in0=cb[:, 0:TN], scalar1=vecs[:, wc:wc + 1], scalar2=None, op0=ALU.mult),
                     reads=[("cu", ci), ("cuh", ci), ("vecs",)], writes=[("acc", ac)])
                for kk in (1, 2):
                    S.op("vector", lambda e, cb=cb, ac=ac, wc=wc, kk=kk: e.scalar_tensor_tensor(
                        out=acc[ac][:, :], in0=cb[:, kk:kk + TN], scalar=vecs[:, wc + kk:wc + kk + 1], in1=acc[ac][:, :],
                        op0=ALU.mult, op1=ALU.add),
                        reads=[("cu", ci), ("cuh", ci), ("vecs",), ("acc", ac)], writes=[("acc", ac)])
                S.op("vector", lambda e, cb=cb, i=i: e.tensor_copy(out=hist_cu[:, i, :], in_=cb[:, TN:TN + 2]),
                     reads=[("cu", ci)], writes=[("hist_cu", i)])
                accs.append(ac)
                tick()
            if last_tile:
                load_win_piece(l + 1, 0)
                load_win_piece(l + 1, 2)
            pull_table(AF.Silu)
            for i in range(4):
                bb = proj_group(l, tt, 4 + i, last_tile)
                ac = accs[i]
                S.op("vector", lambda e, ac=ac, bb=bb: e.tensor_tensor(out=acc[ac][:, :], in0=acc[ac][:, :], in1=banks[bb][:, :], op=ALU.mult),
                     reads=[("bank", bb), ("acc", ac)], writes=[("acc", ac)])
                tick()
            if last_tile:
                load_win_piece(l + 1, 1)
            for i in range(4):
                bg = proj_group(l, tt, 12 + i, last_tile)
                sg = rot("tmp", NTMP)
                ac = accs[i]
                S.op("scalar", lambda e, sg=sg, bg=bg: e.activation(out=tmp[sg][:, :], in_=banks[bg][:, :], func=AF.Silu),
                     reads=[("bank", bg)], writes=[("tmp", sg)])
                S.op("vector", lambda e, sg=sg, ac=ac, i=i: e.tensor_tensor(out=ycat[:, i, :], in0=acc[ac][:, :], in1=tmp[sg][:, :], op=ALU.mult),
                     reads=[("tmp", sg), ("acc", ac)], writes=[("ycat", i)])
                tick()
            if last_tile:
                load_win_piece(l + 1, 3)
            for g in range(4):
                bm = rot("bank", NB)
                S.op("tensor", lambda e, bm=bm, g=g: e.matmul(banks[bm][:, :], lhsT=wpool_sb[:, g, :], rhs=pooled[g][:, :], start=True, stop=True),
                     reads=[("wpool",), ("pooled", g)], writes=[("bank", bm)])
                tick()
                bg = proj_group(l, tt, 20 + g, last_tile)
                sg = rot("tmp", NTMP)
                S.op("scalar", lambda e, sg=sg, bg=bg: e.activation(out=tmp[sg][:, :], in_=banks[bg][:, :], func=AF.Silu),
                     reads=[("bank", bg)], writes=[("tmp", sg)])
                ps = v0 + 52 + g
                S.op("vector", lambda e, bm=bm, g=g, sg=sg, ps=ps: e.scalar_tensor_tensor(
                    out=ycat[:, 4 + g, :], in0=banks[bm][:, :], scalar=vecs[:, ps:ps + 1], in1=tmp[sg][:, :],
                    op0=ALU.mult, op1=ALU.mult),
                    reads=[("bank", bm), ("tmp", sg), ("vecs",)], writes=[("ycat", 4 + g)])
                tick()
            if last_tile:
                load_win_piece(l + 1, 5)
                load_wpool(l + 1)

        def r_phase(l, tt, is_last_layer):
            par = l % 2
            t0 = tt * TN
            xk_idx = list(range(t0 // 256, (t0 + WN) // 256))
            rstd_from(st2, WN)

            def scale(c):
                S.op("vector", lambda e, c=c: e.scalar_tensor_tensor(
                    out=y_sb[:, c, :], in0=y_sb[:, c, :], scalar=G_sb[par][:, c:c + 1], in1=ssb1[:, 0:WN],
                    op0=ALU.mult, op1=ALU.mult),
                    reads=[("y", c), ("G", par), ("ssb1",)], writes=[("y", c)])

            def upd(c):
                xk = [("x", c, si) for si in xk_idx]
                S.op("vector", lambda e, c=c, t0=t0: e.tensor_tensor(
                    out=x_res[:, c, t0:t0 + WN], in0=x_res[:, c, t0:t0 + WN], in1=y_sb[:, c, :], op=ALU.add),
                    reads=[("y", c)] + xk, writes=xk)
                if is_last_layer:
                    S.dma("sync", sl_out, outT[c * 128:(c + 1) * 128, t0:t0 + WN], x_res[:, c, t0:t0 + WN], reads=xk)
            if is_last_layer and tt == NT - 1:
                for c in range(KC):
                    eng = "vector"
                    xk = [("x", c, si) for si in xk_idx]
                    S.op(eng, lambda e, c=c: e.tensor_tensor(out=y_sb[:, c, :], in0=y_sb[:, c, :], in1=ssb1[:, 0:WN], op=ALU.mult),
                         reads=[("y", c), ("ssb1",)], writes=[("y", c)])
                    S.op(eng, lambda e, c=c, t0=t0: e.tensor_tensor(out=x_res[:, c, t0:t0 + WN], in0=x_res[:, c, t0:t0 + WN], in1=y_sb[:, c, :], op=ALU.add),
                         reads=[("y", c)] + xk, writes=xk)
                    S.dma("sync", sl_out, outT[c * 128:(c + 1) * 128, t0:t0 + WN], x_res[:, c, t0:t0 + WN], reads=xk)
            else:
                for c in range(KC):
                    defer(1 + c, lambda c=c: scale(c))
                    defer(12 + c, lambda c=c: upd(c))

        def wout_phase(l, tt, is_last_layer, nxt):
            last_tile = (tt == NT - 1) and (l + 1 < L)
            pend_fin = []
            pull_table(AF.Exp)

            def mm_part(d, b, k0, k1):
                def fn(e, b=b, d=d, k0=k0, k1=k1):
                    ins = None
                    for k in range(k0, k1):
                        ins = e.matmul(banks[b][:, 0:WN], lhsT=wout_sb[:, k, d * 128:(d + 1) * 128], rhs=ycat[:, k, 0:WN],
                                       start=(k == 0), stop=(k == KC - 1))
                    return ins
                S.op("tensor", fn, reads=[("wout", d // 2)] + [("ycat", k) for k in range(k0, k1)], writes=[("bank", b)])

            KS = 5
            pre = {}
            for d in (0, 1):
                pre[d] = rot("bank", NB)
                mm_part(d, pre[d], 0, KS)
            for d in range(KC):
                if d in pre:
                    b = pre[d]
                    mm_part(d, b, KS, KC)
                else:
                    b = rot("bank", NB)
                    mm_part(d, b, 0, KC)
                if last_tile and d % 2 == 1:
                    load_wout_piece(l + 1, d // 2)
                q = rot("xsq", 3)
                S.op("scalar", lambda e, q=q, b=b: e.activation(out=xsq[q][:, 0:WN], in_=banks[b][:, 0:WN], func=AF.Square),
                     reads=[("bank", b)], writes=[("xsq", q)])
                hold("xsq", q)
                if is_last_layer and tt == NT - 1:
                    S.op("scalar", lambda e, b=b, d=d: e.activation(out=y_sb[:, d, :], in_=banks[b][:, 0:WN], func=AF.Copy,
                                                                    scale=G_sb[l % 2][:, d:d + 1]),
                         reads=[("bank", b), ("G", l % 2)], writes=[("y", d)])
                else:
                    S.op("scalar", lambda e, b=b, d=d: e.activation(out=y_sb[:, d, :], in_=banks[b][:, 0:WN], func=AF.Copy),
                         reads=[("bank", b)], writes=[("y", d)])

                def mm(q=q, d=d):
                    S.op("tensor", lambda e, q=q, d=d: e.matmul(st2[:, 0:WN], lhsT=ones_bf[:, :], rhs=xsq[q][:, 0:WN],
                                                                start=(d == 0), stop=(d == KC - 1)),
                         reads=[("xsq", q), ("ones",)], writes=[("st", id(st2))])
                    release("xsq", q)
                    if d == KC - 1:
                        r_phase(l, tt, is_last_layer)
                defer(2, mm)
                if nxt is not None:
                    if pend_fin:
                        pend_fin.pop()()
                    fin = hchunk_split(nxt[0], nxt[1], d)
                    if d == KC - 1:
                        fin()
                    else:
                        pend_fin.append(fin)
                tick()

        S.op("vector", lambda e: e.memset(ones_bf[:, :], 1.0), writes=[("ones",)])
        S.op("vector", lambda e: e.memset(epsb[:, :], float(D * EPS)), writes=[("epsb",)])
        S.op("vector", lambda e: e.memset(dmy_in[:, :], 0.0), writes=[("dmy_in",)])
        S.op("gpsimd", lambda e: e.iota(io16[:, :], [[1, 16]], base=1, channel_multiplier=0, allow_small_or_imprecise_dtypes=True),
             writes=[("io16",)])
        for g in range(4):
            S.op("vector", lambda e, g=g: e.tensor_scalar(out=rvec[:, g, :], in0=io16[:, :], scalar1=float(POOL_W[g]), scalar2=None, op0=ALU.min),
                 reads=[("io16",)], writes=[("rvec0", g)])
        S.op("vector", lambda e: e.reciprocal(out=rvec[:, :, :], in_=rvec[:, :, :]),
             reads=[("rvec0", g) for g in range(4)], writes=[("rvec",)])

        S.dma("sync", sl_c, c_sb[:, :], cT, writes=[("c",)])
        S.dma("sync", sl_vec, vecs[:, :], vecs_d, writes=[("vecs",)])
        xv = xT.rearrange("(k p) t -> p k t", p=128)
        def load_x(tt, after=()):
            S.dma("sync", sl_x[tt], x_res[:, :, tt * TN:(tt + 1) * TN], xv[:, :, tt * TN:(tt + 1) * TN],
                  reads=list(after), writes=[k for c in range(KC) for k in xkeys(c, tt)])
        load_x(0)
        S.op("scalar", lambda e: e.activation(out=cact[:, :], in_=c_sb[:, :], func=AF.Silu), reads=[("c",)], writes=[("cact",)])
        for p in (0, 1, 2, 3):
            src = w_ada[0].rearrange("(k p) c -> p k c", p=128)[:, :, p * 512:(p + 1) * 512]
            S.dma("gpsimd", sl_win[p], win_sb[:, :, p * 512:(p + 1) * 512], src, writes=[("win", p)])
        load_win_piece(0, 4)
        for p in (0, 1, 2, 3):
            for jj in range(4):
                j = 4 * p + jj

                def fn(e, j=j):
                    ins = None
                    for k in range(KC):
                        ins = e.matmul(modps[:, j:j + 1], lhsT=win_sb[:, k, j * 128:(j + 1) * 128],
                                       rhs=cact[:, k:k + 1], start=(k == 0), stop=(k == KC - 1))
                    return ins
                S.op("tensor", fn, reads=[("win", p), ("cact",)], writes=[("modps", j)])
        for p in (0, 2, 1, 3, 5):
            load_win_piece(0, p)
        ada_state["loaded"] = 16
        ada_state["done"] = 16
        if NT > 1:
            load_x(1, after=[("win", 4)])
        load_wada_next()
        load_wada_next()
        load_wpool(0)
        for q in range(4):
            load_wout_piece(0, q)
        for tt in range(2, NT):
            load_x(tt, after=[("wout", 3)])
        mod_finish_AS(0)
        stats1(0, 0)
        hphase(0, 0)

        def mjob():
            n = ada_state["done"]
            if n >= NADA * L:
                return
            mod_piece()
            n += 1
            if n == NADA:
                mod_finish_G(0)
            elif n % NADA == 0:
                mod_finish(n // NADA - 1)

        for l in range(L):
            for tt in range(NT):
                if tt + 1 < NT:
                    nxt = (l, tt + 1)
                elif l + 1 < L:
                    nxt = (l + 1, 0)
                else:
                    nxt = None
                if nxt is not None:
                    stats1_deferred(*nxt)
                target = NADA * min(L, l + 2)
                if l == 0 or l + 1 < L:
                    left_tiles = max(1, (NT - 1) - tt)
                    if tt < NT - 1 or NT == 1:
                        remaining = target - ada_state["done"] - sum(1 for d in deferred if d[2] is mjob)
                        per = -(-remaining // left_tiles)
                        for i in range(per):
                            defer((20 + i) if (l == 0 and tt == 0) else (3 * i + 2), mjob)
                win_phase(l, tt)
                wout_phase(l, tt, l == L - 1, nxt)
        flush()

        S.wait_all("sync", [sl_out])
        S.emit()
    return nc


def _prep_inputs(x, c, w_ada, b_ada, g_pre, w_in, w_conv, w_pool, pool_scale, w_out, g_post, L=DEPTH, T=SEQ):
    f = np.float32
    in_maps = []
    vecs = np.zeros((128, L * VW), dtype=f)
    for l in range(L):
        v0 = l * VW
        vecs[:, v0:v0 + 24] = np.asarray(b_ada[l], f).reshape(24, 128).T
        vecs[:, v0 + 24:v0 + 32] = np.asarray(g_pre[l], f).reshape(8, 128).T
        vecs[:, v0 + 32:v0 + 40] = np.asarray(g_post[l], f).reshape(8, 128).T
        wc = np.asarray(w_conv[l], f)
        for i in range(4):
            vecs[:, v0 + 40 + 3 * i:v0 + 43 + 3 * i] = wc[:, i * 128:(i + 1) * 128].T
        vecs[:, v0 + 52:v0 + 56] = np.asarray(pool_scale[l], f).reshape(4, 128).T
    w_ada = np.ascontiguousarray(np.asarray(w_ada, f)[:L])
    w_in = np.ascontiguousarray(np.asarray(w_in, f)[:L])
    w_out = np.ascontiguousarray(np.asarray(w_out, f)[:L])
    w_pool = np.ascontiguousarray(np.asarray(w_pool, f)[:L])
    nb = x.shape[0]
    for b in range(nb):
        in_maps.append({
            "xT": np.ascontiguousarray(np.asarray(x[b], f)[:T].T),
            "cT": np.ascontiguousarray(np.asarray(c[b], f).reshape(8, 128).T),
            "vecs": vecs,
            "w_ada": w_ada, "w_in": w_in, "w_out": w_out, "w_pool": w_pool,
        })
    return in_maps


_NC_CACHE = {}


def kernel(x, c, w_ada, b_ada, g_pre, w_in, w_conv, w_pool, pool_scale, w_out, g_post):
    in_maps = _prep_inputs(x, c, w_ada, b_ada, g_pre, w_in, w_conv, w_pool, pool_scale, w_out, g_post)
    if "nc" not in _NC_CACHE:
        _NC_CACHE["nc"] = build()
    nc = _NC_CACHE["nc"]
    res = run_bass_kernel_spmd(nc, in_maps, core_ids=list(range(NCORES)))
    out = np.stack([np.ascontiguousarray(r["outT"].T) for r in res.results], axis=0)
    return out.astype(np.float32)
```

```python
import numpy as np
from contextlib import ExitStack

import concourse.bass as bass
import concourse.mybir as mybir
from concourse.bass_utils import run_bass_kernel_spmd

F32 = mybir.dt.float32
BF16 = mybir.dt.bfloat16
AF = mybir.ActivationFunctionType
ALU = mybir.AluOpType

D = 1024
KC = D // 128
NCOLS = 3072
NJ = NCOLS // 128
DEPTH = 4
SEQ = 2048
NCORES = 8
EPS = 1e-6
TN = 512
WN = 512
VW = 56
POOL_W = (2, 4, 8, 16)
NTMP = 3
ADA_W = 128
NADA = NCOLS // ADA_W


class Sched:
    def __init__(self, nc, es):
        self.nc = nc
        self.es = es
        self.prog = {e: [] for e in ("tensor", "vector", "scalar", "gpsimd", "sync")}
        self.sem = {}
        self.count = {}
        for e in self.prog:
            self.sem[e] = es.enter_context(nc.semaphore("s_" + e))
            self.count[e] = 0
        self.waited = {e: {} for e in self.prog}
        self.last_w = {}
        self.readers = {}
        self.nslots = 0
        self.semname = {}

    def slot(self, name):
        s = self.es.enter_context(self.nc.semaphore("d_" + name))
        self.nslots += 1
        return {"sem": s, "count": 0, "name": "d_" + name}

    def _deps(self, eng, reads, writes):
        need = {}

        def add(ev):
            if ev is None:
                return
            name, sem, val = ev
            if name not in need or need[name][1] < val:
                need[name] = (sem, val)

        for k in reads:
            add(self.last_w.get(k))
        for k in writes:
            add(self.last_w.get(k))
            for ev in self.readers.get(k, ()):
                add(ev)
        for name, (sem, val) in need.items():
            if eng == "tensor" and name == "tensor":
                continue
            if self.waited[eng].get(name, 0) < val:
                self.waited[eng][name] = val
                self.prog[eng].append(lambda e, sem=sem, val=val: e.wait_ge(sem, val))

    def _commit(self, ev, reads, writes):
        for k in writes:
            self.last_w[k] = ev
            self.readers[k] = []
        for k in reads:
            self.readers.setdefault(k, []).append(ev)

    def op(self, eng, fn, reads=(), writes=()):
        self._deps(eng, reads, writes)
        self.count[eng] += 1
        sem = self.sem[eng]
        self.prog[eng].append(lambda e, fn=fn, sem=sem: fn(e).then_inc(sem, 1))
        self._commit((eng, sem, self.count[eng]), reads, writes)

    def dma(self, queue, slot, out, in_, reads=(), writes=()):
        self._deps(queue, reads, writes)
        slot["count"] += 16
        sem = slot["sem"]
        self.prog[queue].append(lambda e, out=out, in_=in_, sem=sem: e.dma_start(out=out, in_=in_).then_inc(sem, 16))
        self._commit((slot["name"], sem, slot["count"]), reads, writes)

    def wait_all(self, eng, slots):
        for s in slots:
            if s["count"] > 0:
                self.prog[eng].append(lambda e, sem=s["sem"], v=s["count"]: e.wait_ge(sem, v))

    def emit(self):
        nc = self.nc
        with nc.Block() as block:
            @block.tensor
            def _(e):
                for f in self.prog["tensor"]:
                    f(e)

            @block.vector
            def _(e):
                for f in self.prog["vector"]:
                    f(e)

            @block.scalar
            def _(e):
                for f in self.prog["scalar"]:
                    f(e)

            @block.gpsimd
            def _(e):
                for f in self.prog["gpsimd"]:
                    f(e)

            @block.sync
            def _(e):
                for f in self.prog["sync"]:
                    f(e)


def build(L=DEPTH, T=SEQ):
    NT = T // TN
    nc = bass.Bass("TRN2", target_bir_lowering=False)
    xT = nc.dram_tensor("xT", [D, T], F32, kind="ExternalInput").ap()
    cT = nc.dram_tensor("cT", [128, KC], F32, kind="ExternalInput").ap()
    vecs_d = nc.dram_tensor("vecs", [128, L * VW], F32, kind="ExternalInput").ap()
    w_ada = nc.dram_tensor("w_ada", [L, D, NCOLS], F32, kind="ExternalInput").ap()
    w_in = nc.dram_tensor("w_in", [L, D, NCOLS], F32, kind="ExternalInput").ap()
    w_out = nc.dram_tensor("w_out", [L, D, D], F32, kind="ExternalInput").ap()
    w_pool = nc.dram_tensor("w_pool", [L, 4, 128, 128], F32, kind="ExternalInput").ap()
    outT = nc.dram_tensor("outT", [D, T], F32, kind="ExternalOutput").ap()

    with ExitStack() as es:
        def sb(name, shape, dt):
            return es.enter_context(nc.sbuf_tensor(name, shape, dt))

        def pt(name):
            return es.enter_context(nc.psum_tensor(name, [128, 512], F32))

        S = Sched(nc, es)

        x_res = sb("x_res", [128, KC, T], F32)
        win_sb = sb("win_sb", [128, KC, NCOLS], BF16)
        wout_sb = sb("wout_sb", [128, KC, D], BF16)
        wpool_sb = sb("wpool_sb", [128, 4, 128], BF16)
        wada_sb = [sb(f"wada{i}", [128, KC, ADA_W], BF16) for i in range(2)]
        vecs = sb("vecs_sb", [128, L * VW], F32)
        c_sb = sb("c_sb", [128, KC], F32)
        cact = sb("cact", [128, KC], BF16)
        ones_bf = sb("ones_bf", [128, 128], BF16)
        epsb = sb("epsb", [128, 1], F32)
        dmy_in = sb("dmy_in", [128, 1], F32)
        dmy_out = sb("dmy_out", [128, 1], F32)
        modv = sb("modv", [128, 24], F32)
        A_sb = [sb(f"A{i}", [128, KC], F32) for i in range(2)]
        G_sb = [sb(f"G{i}", [128, KC], F32) for i in range(2)]
        Sh_sb = [sb(f"Sh{i}", [128, KC], F32) for i in range(2)]
        t8 = sb("t8", [128, KC], F32)
        io16 = sb("io16", [128, 16], F32)
        rvec = sb("rvec", [128, 4, 16], F32)
        t16 = sb("t16", [128, 16], F32)
        xsq = [sb(f"xsq{i}", [128, TN], BF16) for i in range(3)]
        ssb1 = sb("ssb1", [128, TN], F32)
        ssb2 = ssb1
        h_sb = sb("h_sb", [128, KC, TN], BF16)
        tmp = [sb(f"tmp{i}", [128, TN], F32) for i in range(NTMP)]
        acc = [sb(f"acc{i}", [128, TN], F32) for i in range(4)]
        cu = [sb(f"cu{i}", [128, 2 + TN], F32) for i in range(2)]
        hist_cu = sb("hist_cu", [128, 4, 2], F32)
        U = [sb(f"U{i}", [128, 15 + TN], F32) for i in range(2)]
        Pb = sb("Pb", [128, 15 + TN], F32)
        Qb = sb("Qb", [128, 15 + TN], F32)
        hist_u = sb("hist_u", [128, 4, 15], F32)
        pooled = [sb(f"pooled{i}", [128, TN], BF16) for i in range(4)]
        ycat = sb("ycat", [128, KC, TN], BF16)
        y_sb = sb("y_sb", [128, KC, WN], F32)
        ysq = xsq

        NB = 6
        banks = [pt(f"pj{i}") for i in range(NB)]
        st1 = pt("st1")
        st2 = st1
        modps = pt("modps")

        sl_x = [S.slot(f"x{i}") for i in range(NT)]
        sl_vec = S.slot("vec")
        sl_c = S.slot("c")
        sl_win = [S.slot(f"win{i}") for i in range(6)]
        sl_wout = [S.slot(f"wout{i}") for i in range(4)]
        sl_wpool = S.slot("wpool")
        sl_wada = [S.slot(f"wada{i}") for i in range(2)]
        sl_out = S.slot("out")

        state = {"bank": 0, "tmp": 0, "xsq": 0, "ysq": 0, "cu": 0, "U": 0}

        busy = {}

        def rot(kind, n):
            i = state[kind]
            state[kind] = (i + 1) % n
            assert not busy.get((kind, i), False), f"rotating buffer {kind}[{i}] re-allocated before its consumer was emitted"
            return i

        def hold(kind, i):
            busy[(kind, i)] = True

        def release(kind, i):
            busy[(kind, i)] = False

        ticks = {"n": 0, "seq": 0}
        deferred = []

        def defer(k, fn):
            ticks["seq"] += 1
            deferred.append((ticks["n"] + k, ticks["seq"], fn))

        def run_due(limit):
            while True:
                due = [d for d in deferred if d[0] <= limit]
                if not due:
                    return
                d = min(due)
                deferred.remove(d)
                d[2]()

        def tick():
            ticks["n"] += 1
            run_due(ticks["n"])

        def flush():
            run_due(1 << 60)

        def pull_table(func):
            S.op("scalar", lambda e, func=func: e.activation(out=dmy_out[:, :], in_=dmy_in[:, :], func=func),
                 reads=[("dmy_in",)], writes=[("dmy_out",)])

        def xkeys(c, tt):
            return [("x", c, 2 * tt), ("x", c, 2 * tt + 1)]

        def load_win_piece(l, p):
            src = w_in[l].rearrange("(k p) c -> p k c", p=128)[:, :, p * 512:(p + 1) * 512]
            S.dma("gpsimd", sl_win[p], win_sb[:, :, p * 512:(p + 1) * 512], src,
                  writes=[("win", p)])

        def load_wout_piece(l, q):
            src = w_out[l].rearrange("(k p) c -> p k c", p=128)[:, :, q * 256:(q + 1) * 256]
            S.dma("gpsimd", sl_wout[q], wout_sb[:, :, q * 256:(q + 1) * 256], src,
                  writes=[("wout", q)])

        def load_wpool(l):
            src = w_pool[l].rearrange("g c d -> c g d")
            S.dma("gpsimd", sl_wpool, wpool_sb[:, :, :], src, writes=[("wpool",)])

        ada_state = {"loaded": 0, "done": 0}

        def load_wada_next():
            n = ada_state["loaded"]
            if n >= NADA * L:
                return
            l, p = divmod(n, NADA)
            slot = n % 2
            src = w_ada[l].rearrange("(k p) c -> p k c", p=128)[:, :, p * ADA_W:(p + 1) * ADA_W]
            S.dma("gpsimd", sl_wada[slot], wada_sb[slot][:, :, :], src, writes=[("wada", slot)])
            ada_state["loaded"] = n + 1

        def mod_piece():
            n = ada_state["done"]
            l, p = divmod(n, NADA)
            slot = n % 2

            def fn(e, slot=slot, p=p):
                ins = None
                for k in range(KC):
                    ins = e.matmul(modps[:, p:p + 1], lhsT=wada_sb[slot][:, k, :],
                                   rhs=cact[:, k:k + 1], start=(k == 0), stop=(k == KC - 1))
                return ins
            S.op("tensor", fn, reads=[("wada", slot), ("cact",)], writes=[("modps", p)])
            ada_state["done"] = n + 1
            load_wada_next()

        def mod_finish_AS(l):
            v0 = l * VW
            par = l % 2
            S.op("vector", lambda e: e.tensor_tensor(out=modv[:, 0:16], in0=modps[:, 0:16], in1=vecs[:, v0:v0 + 16], op=ALU.add),
                 reads=[("modps", p) for p in range(NADA)] + [("vecs",)], writes=[("modv", 0)])
            S.op("vector", lambda e: e.tensor_copy(out=Sh_sb[par][:, :], in_=modv[:, 0:8]),
                 reads=[("modv", 0)], writes=[("Sh", par)])
            S.op("vector", lambda e: e.tensor_scalar(out=t8[:, :], in0=modv[:, 8:16], scalar1=1.0, scalar2=None, op0=ALU.add),
                 reads=[("modv", 0)], writes=[("t8",)])
            S.op("vector", lambda e: e.scalar_tensor_tensor(out=A_sb[par][:, :], in0=t8[:, :], scalar=32.0, in1=vecs[:, v0 + 24:v0 + 32],
                                                            op0=ALU.mult, op1=ALU.mult),
                 reads=[("t8",), ("vecs",)], writes=[("A", par)])

        def mod_finish_G(l):
            v0 = l * VW
            par = l % 2
            S.op("vector", lambda e: e.tensor_tensor(out=modv[:, 16:24], in0=modps[:, 16:24], in1=vecs[:, v0 + 16:v0 + 24], op=ALU.add),
                 reads=[("modps", p) for p in range(NADA)] + [("vecs",)], writes=[("modv", 1)])
            S.op("vector", lambda e: e.scalar_tensor_tensor(out=G_sb[par][:, :], in0=modv[:, 16:24], scalar=32.0, in1=vecs[:, v0 + 32:v0 + 40],
                                                            op0=ALU.mult, op1=ALU.mult),
                 reads=[("modv", 1), ("vecs",)], writes=[("G", par)])

        def mod_finish(l):
            mod_finish_AS(l)
            mod_finish_G(l)

        def rstd_from(st, n, split=False):
            S.op("scalar", lambda e: e.activation(out=ssb1[:, 0:n], in_=st[:, 0:n], func=AF.Ln, bias=epsb[:, 0:1], scale=1.0),
                 reads=[("st", id(st)), ("epsb",)], writes=[("ssb1",)])

            def ex():
                S.op("scalar", lambda e: e.activation(out=ssb1[:, 0:n], in_=ssb1[:, 0:n], func=AF.Exp, scale=-0.5),
                     reads=[("ssb1",)], writes=[("ssb1",)])
            if split:
                defer(1, ex)
            else:
                ex()

        def stats1_sq(tt, c):
            t0 = tt * TN
            b = rot("xsq", 3)
            S.op("scalar", lambda e, b=b, c=c: e.activation(out=xsq[b][:, :], in_=x_res[:, c, t0:t0 + TN], func=AF.Square),
                 reads=xkeys(c, tt), writes=[("xsq", b)])
            hold("xsq", b)
            return b

        def stats1_mm(c, b, split=False):
            S.op("tensor", lambda e, b=b, c=c: e.matmul(st1[:, :], lhsT=ones_bf[:, :], rhs=xsq[b][:, :],
                                                        start=(c == 0), stop=(c == KC - 1)),
                 reads=[("xsq", b), ("ones",)], writes=[("st", id(st1))])
            release("xsq", b)
            if c == KC - 1:
                rstd_from(st1, TN, split)

        def stats1(l, tt):
            for c in range(KC):
                b = stats1_sq(tt, c)
                stats1_mm(c, b)

        def stats1_deferred(l, tt):
            for c in range(KC):
                def job(c=c):
                    b = stats1_sq(tt, c)
                    defer(2, lambda c=c, b=b: stats1_mm(c, b, split=True))
                defer(c + 2, job)

        def hchunk(l, tt, c, on_act=False):
            t0 = tt * TN
            par = l % 2
            ti = rot("tmp", NTMP)
            S.op("vector", lambda e, ti=ti, c=c: e.tensor_tensor(out=tmp[ti][:, :], in0=x_res[:, c, t0:t0 + TN], in1=ssb1[:, :], op=ALU.mult),
                 reads=xkeys(c, tt) + [("ssb1",)], writes=[("tmp", ti)])
            if on_act:
                S.op("scalar", lambda e, ti=ti, c=c: e.activation(out=h_sb[:, c, :], in_=tmp[ti][:, :], func=AF.Identity,
                                                                  bias=Sh_sb[par][:, c:c + 1], scale=A_sb[par][:, c:c + 1]),
                     reads=[("tmp", ti), ("A", par), ("Sh", par)], writes=[("h", c)])
                return
            S.op("vector", lambda e, ti=ti, c=c: e.tensor_scalar(out=h_sb[:, c, :], in0=tmp[ti][:, :],
                                                                 scalar1=A_sb[par][:, c:c + 1], scalar2=Sh_sb[par][:, c:c + 1],
                                                                 op0=ALU.mult, op1=ALU.add),
                 reads=[("tmp", ti), ("A", par), ("Sh", par)], writes=[("h", c)])

        def hphase(l, tt):
            for c in range(KC):
                hchunk(l, tt, c, on_act=True)

        def proj_group(l, tt, j, last_tile):
            b = rot("bank", NB)
            p = j // 4

            def fn(e, b=b, j=j):
                ins = None
                for k in range(KC):
                    ins = e.matmul(banks[b][:, :], lhsT=win_sb[:, k, j * 128:(j + 1) * 128], rhs=h_sb[:, k, :],
                                   start=(k == 0), stop=(k == KC - 1))
                return ins
            S.op("tensor", fn, reads=[("win", p)] + [("h", c) for c in range(KC)], writes=[("bank", b)])
            return b

        def win_phase(l, tt):
            v0 = l * VW
            first = (tt == 0)
            last_tile = (tt == NT - 1) and (l + 1 < L)
            for g in range(4):
                w = POOL_W[g]
                b = proj_group(l, tt, 16 + g, last_tile)
                ui = rot("U", 2)
                Ub = U[ui]
                if first:
                    S.op("vector", lambda e, Ub=Ub: e.memset(Ub[:, 0:15], 0.0), writes=[("U", ui)])
                else:
                    S.op("scalar", lambda e, Ub=Ub, g=g: e.activation(out=Ub[:, 0:15], in_=hist_u[:, g, :], func=AF.Copy),
                         reads=[("hist_u", g)], writes=[("U", ui)])
                S.op("scalar", lambda e, Ub=Ub, b=b: e.activation(out=Ub[:, 15:15 + TN], in_=banks[b][:, :], func=AF.Copy),
                     reads=[("bank", b)], writes=[("Ucur", ui)])
                def save_hist(Ub=Ub, g=g, ui=ui):
                    S.op("scalar", lambda e, Ub=Ub, g=g: e.activation(out=hist_u[:, g, :], in_=Ub[:, TN:TN + 15], func=AF.Copy),
                         reads=[("Ucur", ui)], writes=[("hist_u", g)])
                defer(1, save_hist)
                src, skey = Ub, [("U", ui), ("Ucur", ui)]
                off = 0
                dsts = [(Pb, ("Pb",)), (Qb, ("Qb",))]
                for step in range(g + 1):
                    sh = 1 << step
                    dst, dkey = dsts[step % 2]
                    lo = off + sh
                    S.op("vector", lambda e, dst=dst, src=src, lo=lo, sh=sh: e.tensor_tensor(
                        out=dst[:, lo:15 + TN], in0=src[:, lo:15 + TN], in1=src[:, lo - sh:15 + TN - sh], op=ALU.add),
                        reads=skey, writes=[dkey])
                    src, skey, off = dst, [dkey], lo
                S.op("vector", lambda e, src=src, Ub=Ub, g=g, w=w: e.scalar_tensor_tensor(
                    out=pooled[g][:, :], in0=src[:, 15:15 + TN], scalar=1.0 / w, in1=Ub[:, 15:15 + TN],
                    op0=ALU.mult, op1=ALU.subtract),
                    reads=skey + [("Ucur", ui)], writes=[("pooled", g)])
                if first:
                    S.op("vector", lambda e, src=src, g=g: e.tensor_tensor(out=t16[:, :], in0=src[:, 15:31], in1=rvec[:, g, :], op=ALU.mult),
                         reads=skey + [("rvec",)], writes=[("t16",)])
                    S.op("vector", lambda e, Ub=Ub, g=g: e.tensor_tensor(out=pooled[g][:, 0:16], in0=t16[:, :], in1=Ub[:, 15:31], op=ALU.subtract),
                         reads=[("t16",), ("Ucur", ui)], writes=[("pooled", g)])
                tick()
            if last_tile:
                load_win_piece(l + 1, 4)
            accs = []
            for i in range(4):
                bu = proj_group(l, tt, i, last_tile)
                ua = rot("tmp", NTMP)
                S.op("scalar", lambda e, ua=ua, bu=bu: e.activation(out=tmp[ua][:, :], in_=banks[bu][:, :], func=AF.Copy),
                     reads=[("bank", bu)], writes=[("tmp", ua)])
                tick()
                bc = proj_group(l, tt, 8 + i, last_tile)
                ci = rot("cu", 2)
                cb = cu[ci]
                if first:
                    S.op("vector", lambda e, cb=cb: e.memset(cb[:, 0:2], 0.0), writes=[("cuh", ci)])
                else:
                    S.op("vector", lambda e, cb=cb, i=i: e.tensor_copy(out=cb[:, 0:2], in_=hist_cu[:, i, :]),
                         reads=[("hist_cu", i)], writes=[("cuh", ci)])
                S.op("vector", lambda e, cb=cb, bc=bc, ua=ua: e.tensor_tensor(out=cb[:, 2:2 + TN], in0=banks[bc][:, :], in1=tmp[ua][:, :], op=ALU.mult),
                     reads=[("bank", bc), ("tmp", ua)], writes=[("cu", ci)])
                ac = i
                wc = v0 + 40 + 3 * i
                S.op("vector", lambda e, cb=cb, ac=ac, wc=wc: e.tensor_scalar(out=acc[ac][:, :], in0=cb[:, 0:TN], scalar1=vecs[:, wc:wc + 1], scalar2=None, op0=ALU.mult),
                     reads=[("cu", ci), ("cuh", ci), ("vecs",)], writes=[("acc", ac)])
                for kk in (1, 2):
                    S.op("vector", lambda e, cb=cb, ac=ac, wc=wc, kk=kk: e.scalar_tensor_tensor(
                        out=acc[ac][:, :], in0=cb[:, kk:kk + TN], scalar=vecs[:, wc + kk:wc + kk + 1], in1=acc[ac][:, :],
                        op0=ALU.mult, op1=ALU.add),
                        reads=[("cu", ci), ("cuh", ci), ("vecs",), ("acc", ac)], writes=[("acc", ac)])
                S.op("vector", lambda e, cb=cb, i=i: e.tensor_copy(out=hist_cu[:, i, :], in_=cb[:, TN:TN + 2]),
                     reads=[("cu", ci)], writes=[("hist_cu", i)])
                accs.append(ac)
                tick()
            if last_tile:
                load_win_piece(l + 1, 0)
                load_win_piece(l + 1, 2)
            pull_table(AF.Silu)
            for i in range(4):
                bb = proj_group(l, tt, 4 + i, last_tile)
                ac = accs[i]
                S.op("vector", lambda e, ac=ac, bb=bb: e.tensor_tensor(out=acc[ac][:, :], in0=acc[ac][:, :], in1=banks[bb][:, :], op=ALU.mult),
                     reads=[("bank", bb), ("acc", ac)], writes=[("acc", ac)])
                tick()
            if last_tile:
                load_win_piece(l + 1, 1)
            for i in range(4):
                bg = proj_group(l, tt, 12 + i, last_tile)
                sg = rot("tmp", NTMP)
                ac = accs[i]
                S.op("scalar", lambda e, sg=sg, bg=bg: e.activation(out=tmp[sg][:, :], in_=banks[bg][:, :], func=AF.Silu),
                     reads=[("bank", bg)], writes=[("tmp", sg)])
                S.op("vector", lambda e, sg=sg, ac=ac, i=i: e.tensor_tensor(out=ycat[:, i, :], in0=acc[ac][:, :], in1=tmp[sg][:, :], op=ALU.mult),
                     reads=[("tmp", sg), ("acc", ac)], writes=[("ycat", i)])
                tick()
            if last_tile:
                load_win_piece(l + 1, 3)
            for g in range(4):
                bm = rot("bank", NB)
                S.op("tensor", lambda e, bm=bm, g=g: e.matmul(banks[bm][:, :], lhsT=wpool_sb[:, g, :], rhs=pooled[g][:, :], start=True, stop=True),
                     reads=[("wpool",), ("pooled", g)], writes=[("bank", bm)])
                tick()
                bg = proj_group(l, tt, 20 + g, last_tile)
                sg = rot("tmp", NTMP)
                S.op("scalar", lambda e, sg=sg, bg=bg: e.activation(out=tmp[sg][:, :], in_=banks[bg][:, :], func=AF.Silu),
                     reads=[("bank", bg)], writes=[("tmp", sg)])
                ps = v0 + 52 + g
                S.op("vector", lambda e, bm=bm, g=g, sg=sg, ps=ps: e.scalar_tensor_tensor(
                    out=ycat[:, 4 + g, :], in0=banks[bm][:, :], scalar=vecs[:, ps:ps + 1], in1=tmp[sg][:, :],
                    op0=ALU.mult, op1=ALU.mult),
                    reads=[("bank", bm), ("tmp", sg), ("vecs",)], writes=[("ycat", 4 + g)])
                tick()
            if last_tile:
                load_win_piece(l + 1, 5)
                load_wpool(l + 1)

        def r_phase(l, tt, is_last_layer):
            par = l % 2
            t0 = tt * TN
            xk_idx = list(range(t0 // 256, (t0 + WN) // 256))
            rstd_from(st2, WN, split=not (is_last_layer and tt == NT - 1))

            def scale(c):
                S.op("vector", lambda e, c=c: e.scalar_tensor_tensor(
                    out=y_sb[:, c, :], in0=y_sb[:, c, :], scalar=G_sb[par][:, c:c + 1], in1=ssb1[:, 0:WN],
                    op0=ALU.mult, op1=ALU.mult),
                    reads=[("y", c), ("G", par), ("ssb1",)], writes=[("y", c)])

            def upd(c):
                xk = [("x", c, si) for si in xk_idx]
                S.op("vector", lambda e, c=c, t0=t0: e.tensor_tensor(
                    out=x_res[:, c, t0:t0 + WN], in0=x_res[:, c, t0:t0 + WN], in1=y_sb[:, c, :], op=ALU.add),
                    reads=[("y", c)] + xk, writes=xk)
                if is_last_layer:
                    S.dma("sync", sl_out, outT[c * 128:(c + 1) * 128, t0:t0 + WN], x_res[:, c, t0:t0 + WN], reads=xk)
            if is_last_layer and tt == NT - 1:
                for c in range(KC):
                    eng = "vector"
                    xk = [("x", c, si) for si in xk_idx]
                    S.op(eng, lambda e, c=c: e.tensor_tensor(out=y_sb[:, c, :], in0=y_sb[:, c, :], in1=ssb1[:, 0:WN], op=ALU.mult),
                         reads=[("y", c), ("ssb1",)], writes=[("y", c)])
                    S.op(eng, lambda e, c=c, t0=t0: e.tensor_tensor(out=x_res[:, c, t0:t0 + WN], in0=x_res[:, c, t0:t0 + WN], in1=y_sb[:, c, :], op=ALU.add),
                         reads=[("y", c)] + xk, writes=xk)
                    S.dma("sync", sl_out, outT[c * 128:(c + 1) * 128, t0:t0 + WN], x_res[:, c, t0:t0 + WN], reads=xk)
            else:
                for c in range(KC):
                    defer(1 + c, lambda c=c: scale(c))
                    defer(12 + c, lambda c=c: upd(c))

        def wout_phase(l, tt, is_last_layer, nxt):
            last_tile = (tt == NT - 1) and (l + 1 < L)
            pull_table(AF.Exp)

            def mm_part(d, b, k0, k1):
                def fn(e, b=b, d=d, k0=k0, k1=k1):
                    ins = None
                    for k in range(k0, k1):
                        ins = e.matmul(banks[b][:, 0:WN], lhsT=wout_sb[:, k, d * 128:(d + 1) * 128], rhs=ycat[:, k, 0:WN],
                                       start=(k == 0), stop=(k == KC - 1))
                    return ins
                S.op("tensor", fn, reads=[("wout", d // 2)] + [("ycat", k) for k in range(k0, k1)], writes=[("bank", b)])

            KS = 5
            pre = {}
            for d in (0, 1):
                pre[d] = rot("bank", NB)
                mm_part(d, pre[d], 0, KS)
            for d in range(KC):
                if d in pre:
                    b = pre[d]
                    mm_part(d, b, KS, KC)
                else:
                    b = rot("bank", NB)
                    mm_part(d, b, 0, KC)
                if last_tile and d % 2 == 1:
                    load_wout_piece(l + 1, d // 2)
                q = rot("xsq", 3)
                S.op("scalar", lambda e, q=q, b=b: e.activation(out=xsq[q][:, 0:WN], in_=banks[b][:, 0:WN], func=AF.Square),
                     reads=[("bank", b)], writes=[("xsq", q)])
                hold("xsq", q)
                if is_last_layer and tt == NT - 1:
                    S.op("scalar", lambda e, b=b, d=d: e.activation(out=y_sb[:, d, :], in_=banks[b][:, 0:WN], func=AF.Copy,
                                                                    scale=G_sb[l % 2][:, d:d + 1]),
                         reads=[("bank", b), ("G", l % 2)], writes=[("y", d)])
                else:
                    S.op("scalar", lambda e, b=b, d=d: e.activation(out=y_sb[:, d, :], in_=banks[b][:, 0:WN], func=AF.Copy),
                         reads=[("bank", b)], writes=[("y", d)])

                def mm(q=q, d=d):
                    S.op("tensor", lambda e, q=q, d=d: e.matmul(st2[:, 0:WN], lhsT=ones_bf[:, :], rhs=xsq[q][:, 0:WN],
                                                                start=(d == 0), stop=(d == KC - 1)),
                         reads=[("xsq", q), ("ones",)], writes=[("st", id(st2))])
                    release("xsq", q)
                    if d == KC - 1:
                        r_phase(l, tt, is_last_layer)
                defer(2, mm)
                if nxt is not None:
                    hchunk(nxt[0], nxt[1], d)
                tick()

        S.op("vector", lambda e: e.memset(ones_bf[:, :], 1.0), writes=[("ones",)])
        S.op("vector", lambda e: e.memset(epsb[:, :], float(D * EPS)), writes=[("epsb",)])
        S.op("vector", lambda e: e.memset(dmy_in[:, :], 0.0), writes=[("dmy_in",)])
        S.op("gpsimd", lambda e: e.iota(io16[:, :], [[1, 16]], base=1, channel_multiplier=0, allow_small_or_imprecise_dtypes=True),
             writes=[("io16",)])
        for g in range(4):
            S.op("vector", lambda e, g=g: e.tensor_scalar(out=rvec[:, g, :], in0=io16[:, :], scalar1=float(POOL_W[g]), scalar2=None, op0=ALU.min),
                 reads=[("io16",)], writes=[("rvec0", g)])
        S.op("vector", lambda e: e.reciprocal(out=rvec[:, :, :], in_=rvec[:, :, :]),
             reads=[("rvec0", g) for g in range(4)], writes=[("rvec",)])

        S.dma("sync", sl_c, c_sb[:, :], cT, writes=[("c",)])
        S.dma("sync", sl_vec, vecs[:, :], vecs_d, writes=[("vecs",)])
        xv = xT.rearrange("(k p) t -> p k t", p=128)
        def load_x(tt, after=()):
            S.dma("sync", sl_x[tt], x_res[:, :, tt * TN:(tt + 1) * TN], xv[:, :, tt * TN:(tt + 1) * TN],
                  reads=list(after), writes=[k for c in range(KC) for k in xkeys(c, tt)])
        load_x(0)
        S.op("scalar", lambda e: e.activation(out=cact[:, :], in_=c_sb[:, :], func=AF.Silu), reads=[("c",)], writes=[("cact",)])
        for p in (0, 1, 2, 3):
            src = w_ada[0].rearrange("(k p) c -> p k c", p=128)[:, :, p * 512:(p + 1) * 512]
            S.dma("gpsimd", sl_win[p], win_sb[:, :, p * 512:(p + 1) * 512], src, writes=[("win", p)])
        load_win_piece(0, 4)
        for p in (0, 1, 2, 3):
            for jj in range(4):
                j = 4 * p + jj

                def fn(e, j=j):
                    ins = None
                    for k in range(KC):
                        ins = e.matmul(modps[:, j:j + 1], lhsT=win_sb[:, k, j * 128:(j + 1) * 128],
                                       rhs=cact[:, k:k + 1], start=(k == 0), stop=(k == KC - 1))
                    return ins
                S.op("tensor", fn, reads=[("win", p), ("cact",)], writes=[("modps", j)])
        for p in (0, 2, 1, 3, 5):
            load_win_piece(0, p)
        ada_state["loaded"] = 16
        ada_state["done"] = 16
        if NT > 1:
            load_x(1, after=[("win", 4)])
        load_wada_next()
        load_wada_next()
        load_wpool(0)
        for q in range(4):
            load_wout_piece(0, q)
        for tt in range(2, NT):
            load_x(tt, after=[("wout", 3)])
        mod_finish_AS(0)
        stats1(0, 0)
        hphase(0, 0)

        def mjob():
            n = ada_state["done"]
            if n >= NADA * L:
                return
            mod_piece()
            n += 1
            if n == NADA:
                mod_finish_G(0)
            elif n % NADA == 0:
                mod_finish(n // NADA - 1)

        for l in range(L):
            for tt in range(NT):
                if tt + 1 < NT:
                    nxt = (l, tt + 1)
                elif l + 1 < L:
                    nxt = (l + 1, 0)
                else:
                    nxt = None
                if nxt is not None:
                    stats1_deferred(*nxt)
                target = NADA * min(L, l + 2)
                if l == 0 or l + 1 < L:
                    left_tiles = max(1, (NT - 1) - tt)
                    if tt < NT - 1 or NT == 1:
                        remaining = target - ada_state["done"] - sum(1 for d in deferred if d[2] is mjob)
                        per = -(-remaining // left_tiles)
                        for i in range(per):
                            defer((20 + i) if (l == 0 and tt == 0) else (3 * i + 2), mjob)
                win_phase(l, tt)
                wout_phase(l, tt, l == L - 1, nxt)
        flush()

        S.wait_all("sync", [sl_out])
        S.emit()
    return nc


def _prep_inputs(x, c, w_ada, b_ada, g_pre, w_in, w_conv, w_pool, pool_scale, w_out, g_post, L=DEPTH, T=SEQ):
    f = np.float32
    in_maps = []
    vecs = np.zeros((128, L * VW), dtype=f)
    for l in range(L):
        v0 = l * VW
        vecs[:, v0:v0 + 24] = np.asarray(b_ada[l], f).reshape(24, 128).T
        vecs[:, v0 + 24:v0 + 32] = np.asarray(g_pre[l], f).reshape(8, 128).T
        vecs[:, v0 + 32:v0 + 40] = np.asarray(g_post[l], f).reshape(8, 128).T
        wc = np.asarray(w_conv[l], f)
        for i in range(4):
            vecs[:, v0 + 40 + 3 * i:v0 + 43 + 3 * i] = wc[:, i * 128:(i + 1) * 128].T
        vecs[:, v0 + 52:v0 + 56] = np.asarray(pool_scale[l], f).reshape(4, 128).T
    w_ada = np.ascontiguousarray(np.asarray(w_ada, f)[:L])
    w_in = np.ascontiguousarray(np.asarray(w_in, f)[:L])
    w_out = np.ascontiguousarray(np.asarray(w_out, f)[:L])
    w_pool = np.ascontiguousarray(np.asarray(w_pool, f)[:L])
    nb = x.shape[0]
    for b in range(nb):
        in_maps.append({
            "xT": np.ascontiguousarray(np.asarray(x[b], f)[:T].T),
            "cT": np.ascontiguousarray(np.asarray(c[b], f).reshape(8, 128).T),
            "vecs": vecs,
            "w_ada": w_ada, "w_in": w_in, "w_out": w_out, "w_pool": w_pool,
        })
    return in_maps


_NC_CACHE = {}


def kernel(x, c, w_ada, b_ada, g_pre, w_in, w_conv, w_pool, pool_scale, w_out, g_post):
    in_maps = _prep_inputs(x, c, w_ada, b_ada, g_pre, w_in, w_conv, w_pool, pool_scale, w_out, g_post)
    if "nc" not in _NC_CACHE:
        _NC_CACHE["nc"] = build()
    nc = _NC_CACHE["nc"]
    res = run_bass_kernel_spmd(nc, in_maps, core_ids=list(range(NCORES)))
    out = np.stack([np.ascontiguousarray(r["outT"].T) for r in res.results], axis=0)
    return out.astype(np.float32)
```
